# Optimizing a Trainium2 kernel written in Bass

```python
import math
import jax, jax.numpy as jnp
from jax import lax
import numpy as np

D_MODEL = 1024
BATCH = 1
SEQ = 16384
DEPTH = 4

HEAD_DIM = 64
A_HEADS = 4
B_HEADS = 4
C_HEADS = 8
C_KV_HEADS = 2
A_W = A_HEADS * HEAD_DIM
B_W = B_HEADS * HEAD_DIM
C_QW = C_HEADS * HEAD_DIM
C_KVW = C_KV_HEADS * HEAD_DIM
N_BRANCH = 3
IN_SIZES = (A_W, A_W, A_W, B_W, B_W, B_W, B_HEADS, C_QW, C_KVW, C_KVW, D_MODEL, D_MODEL, D_MODEL)
IN_COLS = sum(IN_SIZES)
MOBA_BLOCK = 256
MOBA_TOPK = 3
Q_BLOCK = 128
WINDOW = 128
NUM_BUCKETS = 32
MAX_DISTANCE = 4096
D_FF = 2816
CONV_WIDTH = 3
EPS = 1e-6
NEG = -1e30
SCALE = HEAD_DIM ** -0.5

kernel_name = "hybrid_moba_fox_swa_gated_block"


def rmsnorm(x, g):
    xf = x.astype(jnp.float32)
    y = xf * lax.rsqrt(jnp.mean(xf * xf, axis=-1, keepdims=True) + EPS)
    return (y * g.astype(jnp.float32)).astype(x.dtype)


def t5_bucket(dist):
    dist = jnp.maximum(dist, 0)
    max_exact = NUM_BUCKETS // 2
    d = jnp.maximum(dist.astype(jnp.float32), 1.0)
    large = max_exact + (jnp.log(d / max_exact) / math.log(MAX_DISTANCE / max_exact)
                         * (NUM_BUCKETS - max_exact)).astype(jnp.int32)
    large = jnp.minimum(large, NUM_BUCKETS - 1)
    return jnp.where(dist < max_exact, dist, large)


def moba_attention(q, k, v, tbl):
    Bn, S, H, Dh = q.shape
    s_pad = -(-S // MOBA_BLOCK) * MOBA_BLOCK
    nb = s_pad // MOBA_BLOCK
    ksel = min(MOBA_TOPK, nb)
    qt = q.transpose(0, 2, 1, 3)
    pad = ((0, 0), (0, 0), (0, s_pad - S), (0, 0))
    k_blocks = jnp.pad(k.transpose(0, 2, 1, 3), pad).reshape(Bn, H, nb, MOBA_BLOCK, Dh)
    v_blocks = jnp.pad(v.transpose(0, 2, 1, 3), pad).reshape(Bn, H, nb, MOBA_BLOCK, Dh)
    k_mean = jnp.mean(k_blocks.astype(jnp.float32), axis=3)
    gate = jnp.einsum('bhsd,bhnd->bhsn', qt.astype(jnp.float32), k_mean)
    q_blk = jnp.arange(S) // MOBA_BLOCK
    past = jnp.arange(nb)[None, :] < q_blk[:, None]
    gate = jnp.where(past, gate, -jnp.inf)
    _, sel = lax.top_k(gate, ksel)
    valid = sel < q_blk[:, None]
    nq = S // Q_BLOCK
    qc = qt.reshape(Bn, H, nq, Q_BLOCK, Dh).transpose(2, 0, 1, 3, 4)
    selc = sel.reshape(Bn, H, nq, Q_BLOCK, ksel).transpose(2, 0, 1, 3, 4)
    validc = valid.reshape(Bn, H, nq, Q_BLOCK, ksel).transpose(2, 0, 1, 3, 4)
    bi = jnp.arange(Bn)[:, None, None, None]
    hi = jnp.arange(H)[None, :, None, None]
    blk_ar = jnp.arange(MOBA_BLOCK)

    def chunk(args):
        c, qb, sb, vb = args
        t = c * Q_BLOCK + jnp.arange(Q_BLOCK)
        kg = k_blocks[bi, hi, sb].reshape(Bn, H, Q_BLOCK, ksel * MOBA_BLOCK, Dh)
        vg = v_blocks[bi, hi, sb].reshape(Bn, H, Q_BLOCK, ksel * MOBA_BLOCK, Dh)
        s_pos = (sb[..., None] * MOBA_BLOCK + blk_ar).reshape(Bn, H, Q_BLOCK, ksel * MOBA_BLOCK)
        lg = jnp.einsum('bhqd,bhqkd->bhqk', qb, kg).astype(jnp.float32) * SCALE
        lg = lg + tbl[hi, t5_bucket(t[:, None] - s_pos)].astype(jnp.float32)
        lg = jnp.where(jnp.repeat(vb, MOBA_BLOCK, axis=-1), lg, NEG)
        own = (c * Q_BLOCK) // MOBA_BLOCK
        ko = lax.dynamic_index_in_dim(k_blocks, own, axis=2, keepdims=False)
        vo = lax.dynamic_index_in_dim(v_blocks, own, axis=2, keepdims=False)
        dist = t[:, None] - (own * MOBA_BLOCK + blk_ar)[None, :]
        lo = jnp.einsum('bhqd,bhkd->bhqk', qb, ko).astype(jnp.float32) * SCALE
        lo = lo + tbl[:, t5_bucket(dist)].astype(jnp.float32)
        lo = jnp.where(dist >= 0, lo, NEG)
        p = jax.nn.softmax(jnp.concatenate([lg, lo], axis=-1), axis=-1)
        pg = p[..., :ksel * MOBA_BLOCK].astype(v.dtype)
        po = p[..., ksel * MOBA_BLOCK:].astype(v.dtype)
        return (jnp.einsum('bhqk,bhqkd->bhqd', pg, vg)
                + jnp.einsum('bhqk,bhkd->bhqd', po, vo))

    out = lax.map(chunk, (jnp.arange(nq), qc, selc, validc))
    return out.transpose(1, 0, 3, 2, 4).reshape(Bn, S, H * Dh)


def forgetting_attention(q, k, v, f_logit):
    Bn, S, H, Dh = q.shape
    logf = jax.nn.log_sigmoid(f_logit.astype(jnp.float32))
    cum = jnp.cumsum(logf, axis=1).transpose(0, 2, 1)
    kt = k.transpose(0, 2, 1, 3)
    vt = v.transpose(0, 2, 1, 3)
    nq = S // Q_BLOCK
    qc = q.transpose(0, 2, 1, 3).reshape(Bn, H, nq, Q_BLOCK, Dh).transpose(2, 0, 1, 3, 4)
    cc = cum.reshape(Bn, H, nq, Q_BLOCK).transpose(2, 0, 1, 3)
    s_pos = jnp.arange(S)

    def block(args):
        i, qb, cb = args
        t = i * Q_BLOCK + jnp.arange(Q_BLOCK)
        lg = jnp.einsum('bhqd,bhkd->bhqk', qb, kt).astype(jnp.float32) * SCALE
        lg = lg + cb[..., None] - cum[:, :, None, :]
        lg = jnp.where(s_pos[None, :] <= t[:, None], lg, NEG)
        p = jax.nn.softmax(lg, axis=-1).astype(vt.dtype)
        return jnp.einsum('bhqk,bhkd->bhqd', p, vt)

    out = lax.map(block, (jnp.arange(nq), qc, cc))
    return out.transpose(1, 0, 3, 2, 4).reshape(Bn, S, H * Dh)


def sliding_window_sink_attention(q, k, v, sinks, tbl):
    Bn, S, Hq, Dh = q.shape
    Hkv = k.shape[2]
    G = Hq // Hkv
    nq = S // Q_BLOCK
    qb = q.reshape(Bn, nq, Q_BLOCK, Hkv, G, Dh)
    kb = k.reshape(Bn, nq, Q_BLOCK, Hkv, Dh)
    vb = v.reshape(Bn, nq, Q_BLOCK, Hkv, Dh)
    shift = ((0, 0), (1, 0), (0, 0), (0, 0), (0, 0))
    kband = jnp.concatenate([jnp.pad(kb, shift)[:, :-1], kb], axis=2)
    vband = jnp.concatenate([jnp.pad(vb, shift)[:, :-1], vb], axis=2)
    lg = jnp.einsum('bnqhgd,bnkhd->bnhgqk', qb, kband).astype(jnp.float32) * SCALE
    tl = jnp.arange(Q_BLOCK)
    sl = jnp.arange(2 * Q_BLOCK)
    dist = tl[:, None] + Q_BLOCK - sl[None, :]
    key_pos = (jnp.arange(nq)[:, None] - 1) * Q_BLOCK + sl[None, :]
    mask = ((dist >= 0) & (dist < WINDOW))[None] & (key_pos >= 0)[:, None, :]
    bias = tbl[:, t5_bucket(dist)].astype(jnp.float32).reshape(Hkv, G, Q_BLOCK, 2 * Q_BLOCK)
    lg = jnp.where(mask[None, :, None, None], lg + bias, NEG)
    sink = jnp.broadcast_to(sinks.astype(jnp.float32).reshape(Hkv, G)[:, :, None, None],
                            lg.shape[:-1] + (1,))
    p = jax.nn.softmax(jnp.concatenate([lg, sink], axis=-1), axis=-1)[..., :-1]
    out = jnp.einsum('bnhgqk,bnkhd->bnqhgd', p.astype(v.dtype), vband)
    return out.reshape(Bn, S, Hq * Dh)


def causal_dwconv(u, w, b):
    C = u.shape[-1]
    y = lax.conv_general_dilated(u, w[:, None, :].astype(u.dtype), window_strides=(1,),
                                 padding=[(CONV_WIDTH - 1, 0)],
                                 dimension_numbers=('NWC', 'WIO', 'NWC'),
                                 feature_group_count=C)
    return y + b


def hybrid_layer(x, ln1, w_in, b_f, sinks, w_pa, w_pb, w_pc, w_o, ln2, w_up, conv_w, conv_b,
                 w_down, tbl_a, tbl_c):
    Bn, S, _ = x.shape
    h = rmsnorm(x, ln1)
    z = h @ w_in
    points = np.cumsum(IN_SIZES)[:-1].tolist()
    qa, ka, va, qb, kb, vb, fb, qc, kc, vc, ga, gb, gc = jnp.split(z, points, axis=-1)

    def heads(t, n):
        return t.reshape(Bn, S, n, HEAD_DIM)

    o_a = moba_attention(heads(qa, A_HEADS), heads(ka, A_HEADS), heads(va, A_HEADS), tbl_a)
    o_b = forgetting_attention(heads(qb, B_HEADS), heads(kb, B_HEADS), heads(vb, B_HEADS), fb + b_f)
    o_c = sliding_window_sink_attention(heads(qc, C_HEADS), heads(kc, C_KV_HEADS),
                                        heads(vc, C_KV_HEADS), sinks, tbl_c)
    merged = (jax.nn.sigmoid(ga) * (o_a @ w_pa)
              + jax.nn.sigmoid(gb) * (o_b @ w_pb)
              + jax.nn.sigmoid(gc) * (o_c @ w_pc))
    x = x + merged @ w_o

    h2 = rmsnorm(x, ln2)
    u = causal_dwconv(h2 @ w_up, conv_w, conv_b)
    a, b = jnp.split(u, 2, axis=-1)
    return x + (jax.nn.silu(a) * b) @ w_down


def setup_inputs(seed: int = 0) -> dict:
    key = jax.random.key(seed)
    ks = jax.random.split(key, 16)

    def nrm(k, shape, scale):
        return jax.random.normal(k, shape, jnp.float32) * scale

    return {
        "x": nrm(ks[0], (BATCH, SEQ, D_MODEL), 1.0),
        "ln1": 1.0 + nrm(ks[1], (DEPTH, D_MODEL), 0.05),
        "w_in": nrm(ks[2], (DEPTH, D_MODEL, IN_COLS), D_MODEL ** -0.5),
        "b_f": 4.0 + nrm(ks[3], (DEPTH, B_HEADS), 0.5),
        "sinks": nrm(ks[4], (DEPTH, C_HEADS), 1.0),
        "w_pa": nrm(ks[5], (DEPTH, A_W, D_MODEL), A_W ** -0.5),
        "w_pb": nrm(ks[6], (DEPTH, B_W, D_MODEL), B_W ** -0.5),
        "w_pc": nrm(ks[7], (DEPTH, C_QW, D_MODEL), C_QW ** -0.5),
        "w_o": nrm(ks[8], (DEPTH, D_MODEL, D_MODEL), D_MODEL ** -0.5),
        "ln2": 1.0 + nrm(ks[9], (DEPTH, D_MODEL), 0.05),
        "w_up": nrm(ks[10], (DEPTH, D_MODEL, 2 * D_FF), D_MODEL ** -0.5),
        "conv_w": nrm(ks[11], (DEPTH, CONV_WIDTH, 2 * D_FF), CONV_WIDTH ** -0.5),
        "conv_b": nrm(ks[12], (DEPTH, 2 * D_FF), 0.02),
        "w_down": nrm(ks[13], (DEPTH, D_FF, D_MODEL), D_FF ** -0.5),
        "rel_bias": nrm(ks[14], (NUM_BUCKETS, A_HEADS + C_HEADS), 0.5),
        "ln_f": 1.0 + nrm(ks[15], (D_MODEL,), 0.05),
    }


def reference(x, ln1, w_in, b_f, sinks, w_pa, w_pb, w_pc, w_o, ln2, w_up, conv_w, conv_b,
              w_down, rel_bias, ln_f):
    tbl_a = rel_bias[:, :A_HEADS].T
    tbl_c = rel_bias[:, A_HEADS:].T
    for l in range(DEPTH):
        x = hybrid_layer(x, ln1[l], w_in[l], b_f[l], sinks[l], w_pa[l], w_pb[l], w_pc[l], w_o[l],
                         ln2[l], w_up[l], conv_w[l], conv_b[l], w_down[l], tbl_a, tbl_c)
    return rmsnorm(x, ln_f)
```

```python
import math
from contextlib import ExitStack

import numpy as np
import ml_dtypes
import concourse.bass as bass
import concourse.mybir as mybir
from concourse.bass_utils import run_bass_kernel_spmd

F32 = mybir.dt.float32
BF16 = mybir.dt.bfloat16
AF = mybir.ActivationFunctionType
ALU = mybir.AluOpType
AX = mybir.AxisListType
NPBF = ml_dtypes.bfloat16

S = 16384
D = 1024
NCORE = 8
DEPTH = 4
CH = 256
NJ = 8
TPC = 2048
DFF = 2816
EPS = 1e-6
SCALE = 0.125
NEGM = -30000.0
IMAX = 24
STRIPW = 128 * (IMAX + 15) + 256


class Buf:
    __slots__ = ("name", "writer", "readers", "sem", "ndma", "last_dma", "excl")

    def __init__(self, name):
        self.name = name
        self.excl = False
        self.writer = None
        self.readers = []
        self.sem = None
        self.ndma = 0
        self.last_dma = None


class Ins:
    __slots__ = ("eng", "fn", "deps", "signal", "sem", "val", "is_dma", "sembuf", "gid")

    def __init__(self, eng, fn, is_dma, sembuf):
        self.eng = eng
        self.fn = fn
        self.deps = []
        self.signal = False
        self.sem = None
        self.val = 0
        self.is_dma = is_dma
        self.sembuf = sembuf


class Prog:
    ENGS = ("pe", "act", "dve", "pool", "sp")

    def __init__(self, nc, stack):
        self.nc = nc
        self.stack = stack
        self.streams = {e: [] for e in self.ENGS}
        self.n = 0
        self.nbuf = 0
        self.final_dmas = []
        self.nbank = 0

    def buf(self, name=None):
        self.nbuf += 1
        return Buf(name or f"b{self.nbuf}")

    def bufs(self, n, name="b"):
        return [self.buf(f"{name}{i}") for i in range(n)]

    def sbuf(self, name, shape, dtype):
        return self.stack.enter_context(self.nc.sbuf_tensor(name, list(shape), dtype))

    def psum(self, name, shape, dtype=F32):
        return self.stack.enter_context(self.nc.psum_tensor(name, list(shape), dtype))

    def op(self, eng, fn, reads=(), writes=(), dma=False, sembuf=None):
        ins = Ins(eng, fn, dma, sembuf)
        ins.gid = self.n
        self.n += 1
        deps = {}
        xr = [b for b in reads if b.excl]
        if xr:
            reads = [b for b in reads if not b.excl]
            writes = list(writes) + xr

        def add(d, raw):
            if d is None or d is ins:
                return
            if not d.is_dma and not dma and d.eng == eng:
                if eng == "pe" or not raw:
                    return
            deps[id(d)] = d

        for b in reads:
            add(b.writer, True)
        for b in writes:
            add(b.writer, False)
            for r in b.readers:
                add(r, False)
        if dma:
            add(sembuf.last_dma, False)
            sembuf.last_dma = ins
        for b in writes:
            b.writer = ins
            b.readers = []
        for b in reads:
            b.readers.append(ins)
        for d in deps.values():
            d.signal = True
        ins.deps = list(deps.values())
        self.streams[eng].append(ins)
        return ins

    def dma(self, eng, out, in_, reads=(), writes=(), sembuf=None, final=False):
        if sembuf is None:
            sembuf = (list(writes) + list(reads))[0]
        i = self.op(eng, lambda e: e.dma_start(out=out, in_=in_), reads=reads, writes=writes,
                    dma=True, sembuf=sembuf)
        if final:
            i.signal = True
            self.final_dmas.append(i)
        return i

    def emit(self):
        nc = self.nc
        stack = self.stack
        engsem = {e: stack.enter_context(nc.semaphore(f"sem_{e}")) for e in self.ENGS}
        cnt = {e: 0 for e in self.ENGS}
        alli = []
        for e in self.ENGS:
            alli.extend(self.streams[e])
        alli.sort(key=lambda i: i.gid)
        nsem = len(self.ENGS)
        for ins in alli:
            e = ins.eng
            if ins.is_dma:
                b = ins.sembuf
                if b.sem is None:
                    b.sem = stack.enter_context(nc.semaphore(f"sd_{b.name}"))
                    nsem += 1
                b.ndma += 1
                ins.sem = b.sem
                ins.val = 16 * b.ndma
            elif ins.signal:
                cnt[e] += 1
                ins.sem = engsem[e]
                ins.val = cnt[e]
        self.stats = dict(nsem=nsem, cnt=cnt, n=self.n,
                          per_eng={e: len(self.streams[e]) for e in self.ENGS})

        def run_stream(ename, eng, extra_final=()):
            waited = {}
            for ins in self.streams[ename]:
                for d in ins.deps:
                    key = id(d.sem)
                    if waited.get(key, 0) < d.val:
                        eng.wait_ge(d.sem, d.val)
                        waited[key] = d.val
                r = ins.fn(eng)
                if ins.is_dma:
                    r.then_inc(ins.sem, 16)
                elif ins.signal:
                    r.then_inc(ins.sem, 1)
            for d in extra_final:
                key = id(d.sem)
                if waited.get(key, 0) < d.val:
                    eng.wait_ge(d.sem, d.val)
                    waited[key] = d.val

        with nc.Block() as block:
            @block.tensor
            def _(eng):
                run_stream("pe", eng)

            @block.scalar
            def _(eng):
                run_stream("act", eng)

            @block.vector
            def _(eng):
                run_stream("dve", eng)

            @block.gpsimd
            def _(eng):
                run_stream("pool", eng)

            @block.sync
            def _(eng):
                run_stream("sp", eng, extra_final=self.final_dmas)


class KB:
    def __init__(self, name):
        self.name = name
        self.nc = bass.Bass("TRN2", target_bir_lowering=False)
        self.stack = ExitStack()
        self.P = Prog(self.nc, self.stack)
        self.ins = {}
        self.outs = {}
        P = self.P
        self.ps = [P.psum(f"ps{i}", [128, 512]) for i in range(8)]
        self.Bps = P.bufs(8, "ps")
        for b in self.Bps:
            b.excl = True
        self.rr = {}

    def din(self, name, shape, dtype=F32):
        t = self.nc.dram_tensor(name, list(shape), dtype, kind="ExternalInput").ap()
        self.ins[name] = (tuple(shape), dtype)
        return t

    def dout(self, name, shape, dtype=F32):
        t = self.nc.dram_tensor(name, list(shape), dtype, kind="ExternalOutput").ap()
        self.outs[name] = (tuple(shape), dtype)
        return t

    def bank(self, group="g", banks=(0, 1, 2, 3, 4, 5, 6, 7)):
        k = self.rr.get(group, 0)
        self.rr[group] = k + 1
        return banks[k % len(banks)]

    def finish(self):
        self.P.emit()
        self.stack.close()
        return self.nc


def npdt(dt):
    return np.float32 if dt == F32 else NPBF


def setup_common(K, need_x_in=True):
    P = K.P
    C = K
    C.xT = P.sbuf("xT_sb", [128, 8, TPC], F32)
    C.BxT = [[P.buf(f"xT{ft}_{tg}") for tg in range(4)] for ft in range(8)]
    C.A1 = P.sbuf("A1", [128, 16384], BF16)
    C.A2 = P.sbuf("A2", [128, 16384], BF16)
    C.onesb = P.sbuf("onesb", [128, 128], BF16)
    C.Bconst = P.buf("const")
    P.op("dve", lambda e: e.memset(C.onesb[:], 1.0), writes=[C.Bconst])
    C.sq = P.sbuf("sq", [128, 8, 512], BF16)
    C.Bsq = P.buf("sq")
    C.rstd = P.sbuf("rstd", [128, 512], F32)
    C.Brstd = P.buf("rstd")
    C.rstd2 = P.sbuf("rstd2", [128, 512], F32)
    C.Brstd2 = P.buf("rstd2")
    if need_x_in:
        xin = K.din("xT", [D, TPC], F32)
        xv = xin.rearrange("(kc p) t -> p kc t", p=128)
        for tg in range(4):
            P.dma("sp", C.xT[:, :, tg * 512:(tg + 1) * 512], xv[:, :, tg * 512:(tg + 1) * 512],
                  writes=[C.BxT[ft][tg] for ft in range(8)], sembuf=C.BxT[0][tg])


def load_small(K, name, shape, dtype=F32, eng="sp"):
    P = K.P
    d = K.din(name, shape, dtype)
    t = P.sbuf(name + "_sb", shape, dtype)
    b = P.buf(name)
    if len(shape) == 2:
        P.dma(eng, t[:], d[:, :], writes=[b])
    elif len(shape) == 3:
        P.dma(eng, t[:], d[:, :, :], writes=[b])
    else:
        P.dma(eng, t[:], d[:, :, :, :], writes=[b])
    return t, b


def rmsnorm_tg(K, tg, gamma, Bgamma, hT, BhT, hslice, extra_w=()):
    P = K.P
    sl = slice(tg * 512, (tg + 1) * 512)
    xr = [K.BxT[ft][tg] for ft in range(8)]
    P.op("act", lambda e: e.activation(out=K.sq[:], in_=K.xT[:, :, sl], func=AF.Square),
         reads=xr, writes=[K.Bsq])
    b = K.bank()
    for kc in range(8):
        P.op("pe", lambda e, kc=kc: e.matmul(K.ps[b][:], K.onesb[:], K.sq[:, kc, :],
                                              start=(kc == 0), stop=(kc == 7)),
             reads=[K.Bsq, K.Bconst], writes=[K.Bps[b]])
    P.op("act", lambda e: e.activation(out=K.rstd2[:], in_=K.ps[b][:], func=AF.Sqrt,
                                        bias=K.epscol[:, 0:1], scale=1.0 / D),
         reads=[K.Bps[b], K.Bconst2], writes=[K.Brstd2])
    P.op("dve", lambda e: e.reciprocal(out=K.rstd[:], in_=K.rstd2[:]), reads=[K.Brstd2], writes=[K.Brstd])
    for kc in range(8):
        P.op("dve", lambda e, kc=kc: e.scalar_tensor_tensor(
            out=hT[:, kc, hslice], in0=K.xT[:, kc, sl], scalar=gamma[:, kc:kc + 1], in1=K.rstd[:],
            op0=ALU.mult, op1=ALU.mult),
            reads=[K.BxT[kc][tg], K.Brstd, Bgamma], writes=[BhT] + list(extra_w))


def setup_eps(K):
    P = K.P
    K.epscol = P.sbuf("epscol", [128, 1], F32)
    K.Bconst2 = P.buf("const2")
    P.op("dve", lambda e: e.memset(K.epscol[:], EPS), writes=[K.Bconst2])


NFM = 13
NTM = 644


def phase_p1(K, sfx="", extra_w=()):
    P = K.P
    ln1, Bln1 = load_small(K, "ln1" + sfx, [128, 8])
    bfb, Bbfb = load_small(K, "bfb" + sfx, [128, 4])
    wfm = K.din("wfm" + sfx, [D, NFM * 128], F32).rearrange("(kc p) n -> p kc n", p=128)
    wtm = K.din("wtm" + sfx, [D, NTM], F32).rearrange("(kc p) n -> p kc n", p=128)
    qkT = K.dout("qkT", [NFM * 128, TPC], BF16)
    vtm = K.dout("vtm", [TPC, 640], BF16)
    lf = K.dout("lf", [128, 64], F32)
    kms = K.dout("kms", [128, 16], F32)

    hT = K.A1[:, :].rearrange("p (kc t) -> p kc t", kc=8)
    BhT = P.bufs(4, "hT")
    for tg in range(4):
        rmsnorm_tg(K, tg, ln1, Bln1, hT, BhT[tg], slice(tg * 512, (tg + 1) * 512), extra_w=extra_w)

    wtm_sb = P.sbuf("wtm_sb", [128, 8, NTM], BF16)
    Bwtm = P.buf("wtm")
    P.dma("pool", wtm_sb[:], wtm[:, :, :], writes=[Bwtm])
    wq = [P.sbuf(f"wq{i}", [128, 8, 128], BF16) for i in range(2)]
    Bwq = P.bufs(2, "wq")
    stg = [P.sbuf(f"stg{i}", [128, 512], BF16) for i in range(3)]
    Bstg = P.bufs(3, "stg")
    kmsum = P.sbuf("kmsum", [128, 16], F32)
    Bkm = P.buf("kmsum")
    nst = 0
    for m in range(NFM):
        w = wq[m % 2]
        bw = Bwq[m % 2]
        P.dma("pool", w[:], wfm[:, :, m * 128:(m + 1) * 128], writes=[bw])
        is_q = m in (0, 1, 4, 5, 8, 9, 10, 11)
        for tg in range(4):
            b = K.bank()
            for kc in range(8):
                P.op("pe", lambda e, kc=kc, b=b, w=w, tg=tg: e.matmul(
                    K.ps[b][:], w[:, kc, :], hT[:, kc, tg * 512:(tg + 1) * 512],
                    start=(kc == 0), stop=(kc == 7)), reads=[bw, BhT[tg]], writes=[K.Bps[b]])
            s = stg[nst % 3]
            bs = Bstg[nst % 3]
            nst += 1
            P.op("act", lambda e, s=s, b=b, is_q=is_q: e.activation(
                out=s[:], in_=K.ps[b][:], func=AF.Copy, scale=(SCALE if is_q else 1.0)),
                reads=[K.Bps[b]], writes=[bs])
            if m in (2, 3):
                P.op("dve", lambda e, b=b, m=m, tg=tg: e.tensor_reduce(
                    out=kmsum[:, (m - 2) * 8 + tg * 2:(m - 2) * 8 + tg * 2 + 2],
                    in_=K.ps[b][:].rearrange("p (c t) -> p c t", c=2), axis=AX.X, op=ALU.add),
                    reads=[K.Bps[b]], writes=[Bkm])
            P.dma("sp", qkT[m * 128:(m + 1) * 128, tg * 512:(tg + 1) * 512], s[:], reads=[bs], final=True)
    P.op("dve", lambda e: e.tensor_scalar(out=kmsum[:], in0=kmsum[:], scalar1=1.0 / CH, scalar2=None,
                                           op0=ALU.mult), reads=[Bkm], writes=[Bkm])
    P.dma("sp", kms[:, :], kmsum[:], reads=[Bkm], final=True)

    vst = [P.sbuf(f"vst{i}", [128, 640], BF16) for i in range(2)]
    Bvst = P.bufs(2, "vst")
    fl = P.sbuf("fl", [128, 16, 4], F32)
    Bfl = P.buf("fl")
    for tt in range(16):
        tg = tt // 4
        b1 = K.bank()
        b2 = K.bank()
        for (b, c0, c1) in ((b1, 0, 512), (b2, 512, NTM)):
            for kc in range(8):
                P.op("pe", lambda e, kc=kc, b=b, c0=c0, c1=c1, tt=tt: e.matmul(
                    K.ps[b][:, 0:c1 - c0], hT[:, kc, tt * 128:(tt + 1) * 128], wtm_sb[:, kc, c0:c1],
                    start=(kc == 0), stop=(kc == 7)), reads=[Bwtm, BhT[tg]], writes=[K.Bps[b]])
        v = vst[tt % 2]
        bv = Bvst[tt % 2]
        P.op("act", lambda e, v=v, b1=b1: e.activation(out=v[:, 0:512], in_=K.ps[b1][:], func=AF.Copy),
             reads=[K.Bps[b1]], writes=[bv])
        P.op("dve", lambda e, v=v, b2=b2: e.tensor_copy(out=v[:, 512:640], in_=K.ps[b2][:, 0:128]),
             reads=[K.Bps[b2]], writes=[bv])
        P.op("dve", lambda e, b2=b2, tt=tt: e.tensor_tensor(out=fl[:, tt, :], in0=K.ps[b2][:, 128:132],
                                                            in1=bfb[:], op=ALU.add),
             reads=[K.Bps[b2], Bbfb], writes=[Bfl])
        P.dma("sp", vtm[tt * 128:(tt + 1) * 128, :], v[:], reads=[bv], final=True)
    fl2 = fl[:].rearrange("p a b -> p (a b)")
    P.op("act", lambda e: e.activation(out=fl2, in_=fl2, func=AF.Exp, scale=-1.0), reads=[Bfl], writes=[Bfl])
    P.op("act", lambda e: e.activation(out=fl2, in_=fl2, func=AF.Ln, bias=K.onecol[:, 0:1], scale=1.0),
         reads=[Bfl, K.Bconst2], writes=[Bfl])
    P.dma("sp", lf[:, :], fl2, reads=[Bfl], final=True)


def setup_onecol(K):
    P = K.P
    K.onecol = P.sbuf("onecol", [128, 1], F32)
    P.op("dve", lambda e: e.memset(K.onecol[:], 1.0), writes=[K.Bconst2])


def build_k0():
    K = KB("k0")
    setup_common(K)
    setup_eps(K)
    setup_onecol(K)
    phase_p1(K)
    return K


_CACHE = {}


def get_kernel(name, builder):
    if name not in _CACHE:
        K = builder()
        K.finish()
        _CACHE[name] = K
    return _CACHE[name]


def run(K, in_maps):
    for m in in_maps:
        for k, (shape, dt) in K.ins.items():
            a = m[k]
            assert tuple(a.shape) == shape, (k, a.shape, shape)
            assert a.dtype == npdt(dt), (k, a.dtype)
    res = run_bass_kernel_spmd(K.nc, in_maps, core_ids=list(range(NCORE)))
    return res.results


def tok_perm(c):
    j = np.arange(NJ)[:, None]
    r = np.arange(CH)[None, :]
    return (CH * (NJ * j + c) + r).reshape(-1)


O_QA, O_KA, O_VA, O_QB, O_KB, O_VB, O_F, O_QC, O_KC, O_VC, O_GA, O_GB, O_GC = (
    0, 256, 512, 768, 1024, 1280, 1536, 1540, 2052, 2180, 2308, 3332, 4356)


def p1_inputs(l, inp):
    w = inp["w_in"][l]
    fm_cols = np.concatenate([np.arange(O_QA, O_QA + 256), np.arange(O_KA, O_KA + 256),
                              np.arange(O_QB, O_QB + 256), np.arange(O_KB, O_KB + 256),
                              np.arange(O_QC, O_QC + 512), np.arange(O_KC, O_KC + 128)])
    tm_cols = np.concatenate([np.arange(O_VA, O_VA + 256), np.arange(O_VB, O_VB + 256),
                              np.arange(O_VC, O_VC + 128), np.arange(O_F, O_F + 4)])
    return dict(
        wfm=np.ascontiguousarray(w[:, fm_cols]),
        wtm=np.ascontiguousarray(w[:, tm_cols]),
        ln1=np.ascontiguousarray(inp["ln1"][l].reshape(8, 128).T),
        bfb=np.ascontiguousarray(np.broadcast_to(inp["b_f"][l][None, :], (128, 4))),
    )


def build_ka():
    K = KB("ka")
    P = K.P
    K.XA = P.sbuf("XA", [128, 16384], F32)
    K.xT = K.XA[:, :].rearrange("p (kc t) -> p kc t", kc=8)
    K.BxT = [[P.buf(f"xT{ft}_{tg}") for tg in range(4)] for ft in range(8)]
    A1 = K.XA[:, 0:8192].bitcast(BF16)
    A2 = K.XA[:, 8192:16384].bitcast(BF16)
    BA1 = P.buf("A1")
    BA2 = P.buf("A2")
    A2v = A2.rearrange("p (t c) -> p t c", c=128)
    OT = P.sbuf("OT", [128, 8, TPC], BF16)
    BOT = P.bufs(8, "OT")
    K.Bconst = P.buf("const")
    K.onesb = P.sbuf("onesb", [128, 128], BF16)
    P.op("dve", lambda e: e.memset(K.onesb[:], 1.0), writes=[K.Bconst])
    onesf = P.sbuf("onesf", [128, 128], F32)
    P.op("dve", lambda e: e.memset(onesf[:], 1.0), writes=[K.Bconst])
    setup_eps(K)
    zcol = P.sbuf("zcol", [128, 1], F32)
    P.op("dve", lambda e: e.memset(zcol[:], 0.0), writes=[K.Bconst2])
    P.op("dve", lambda e: e.memset(A2v[:, :, 64:128], 1.0), writes=[BA2])

    ident, Bid = load_small(K, "ident", [128, 128], BF16)
    utri, Butri = load_small(K, "utri", [128, 128], F32)
    OH, BOH = load_small(K, "OH", [128, 8, 128], F32)
    pm, Bpm = load_small(K, "pm", [128, 8, 64], F32)
    own, Bown = load_small(K, "own", [128, 8, 64], F32)
    DM, BDM = load_small(K, "DM", [128, 16, 256], BF16)
    b31, Bb31 = load_small(K, "b31", [128, 4], F32)
    sinkb, Bsink = load_small(K, "sinkb", [128, 8], F32)
    lfT, BlfT = load_small(K, "lfT", [128, 512], F32)
    kmf, Bkmf = load_small(K, "kmT", [64, 4, 64], F32)
    ind = K.din("ind", [64, S], BF16)
    P.dma("sp", A1[64:128, :], ind[:, :], writes=[BA1])
    kmb = P.sbuf("kmb", [64, 4, 64], BF16)
    Bkmb = P.buf("kmb")
    P.op("dve", lambda e: e.tensor_copy(out=kmb[:], in_=kmf[:]), reads=[Bkmf], writes=[Bkmb])
    esink = P.sbuf("esink", [128, 8], F32)
    Besink = P.buf("esink")
    P.op("act", lambda e: e.activation(out=esink[:], in_=sinkb[:], func=AF.Exp), reads=[Bsink], writes=[Besink])

    qa = K.din("qa", [4, 64, TPC], BF16)
    qb = K.din("qb", [4, 64, TPC], BF16)
    qc = K.din("qc", [8, 64, TPC], BF16)
    kaT = K.din("kaT", [4, 64, S], BF16)
    kbT = K.din("kbT", [4, 64, S], BF16)
    va = K.din("va", [4, 128, 128, 64], BF16)
    vb = K.din("vb", [4, 128, 128, 64], BF16)
    kch = K.din("kch", [2, 64, 8, 384], BF16)
    vch = K.din("vch", [2, 128, 8, 3, 64], BF16)
    strip = K.din("strip", [4, 128, STRIPW], BF16)
    BCd = K.din("BC", [8, 128, 3, 256], BF16)
    BC0d = K.din("BC0", [8, 128, 256], BF16)

    CLT = P.sbuf("CLT", [128, 512], F32)
    BCLT = P.buf("CLT")
    Et = P.sbuf("Et", [128, 512], F32)
    BEt = P.buf("Et")
    bw = K.bank()
    P.op("pe", lambda e: e.matmul(K.ps[bw][:], utri[:], lfT[:], start=True, stop=True),
         reads=[Butri, BlfT], writes=[K.Bps[bw]])
    bt = K.bank()
    P.op("pe", lambda e: e.matmul(K.ps[bt][:], onesf[:], lfT[:], start=True, stop=True),
         reads=[K.Bconst, BlfT], writes=[K.Bps[bt]])
    tot = P.sbuf("tot", [128, 512], F32)
    Btot = P.buf("tot")
    P.op("dve", lambda e: e.tensor_copy(out=tot[:], in_=K.ps[bt][:]), reads=[K.Bps[bt]], writes=[Btot])
    totv = tot[:].rearrange("p (t h) -> p h t", h=4)
    Etv = Et[:].rearrange("p (t h) -> p h t", h=4)
    for h in range(4):
        P.op("dve", lambda e, h=h: e.tensor_tensor_scan(
            out=Etv[:, h, :], data0=onesf[:, 0:128], data1=totv[:, h, :], initial=0.0,
            op0=ALU.mult, op1=ALU.add), reads=[Btot, K.Bconst], writes=[BEt])
    P.op("dve", lambda e: e.tensor_tensor(out=Et[:], in0=Et[:], in1=tot[:], op=ALU.subtract),
         reads=[BEt, Btot], writes=[BEt])
    P.op("dve", lambda e: e.tensor_tensor(out=CLT[:], in0=K.ps[bw][:], in1=Et[:], op=ALU.add),
         reads=[K.Bps[bw], BEt], writes=[BCLT])
    clref = P.sbuf("clref", [128, 32], F32)
    Bclref = P.buf("clref")
    scr = P.sbuf("scr", [128, 128], F32)
    Bscr = P.buf("scr")
    for j in range(8):
        for h in range(4):
            P.op("dve", lambda e, j=j, h=h: e.tensor_tensor(out=scr[:], in0=Etv[:, h, :], in1=OH[:, j, :], op=ALU.mult),
                 reads=[BEt, BOH], writes=[Bscr])
            P.op("dve", lambda e, j=j, h=h: e.tensor_reduce(out=clref[:, j * 4 + h:j * 4 + h + 1], in_=scr[:],
                                                            axis=AX.X, op=ALU.add),
                 reads=[Bscr], writes=[Bclref])
    CLTv = CLT[:].rearrange("p (t h) -> p h t", h=4)

    NPT = 6
    PT = [P.sbuf(f"PT{i}", [128, 256], BF16) for i in range(NPT)]
    BPT = P.bufs(NPT, "PT")
    rc = P.sbuf("rc", [128, 256], F32)
    Brc = P.buf("rc")
    SB = (0, 1, 2, 3)
    OB = (4, 5)
    st = dict(npt=0)

    def attn_chunk(tiles, qrhs, qreads, vfn, vreads, o_dst, o_buf, hpar, extra_den=None):
        bo = K.bank("o", OB)
        n = len(tiles)
        sbank = [None] * n
        ptl = [None] * n

        def issue_s(t):
            T = tiles[t]
            b = K.bank("s", SB)
            sbank[t] = b
            hasb = T.get("bias") is not None
            P.op("pe", lambda e: e.matmul(K.ps[b][:, 0:256], T["lhsT"], qrhs, start=True, stop=not hasb),
                 reads=list(T["kreads"]) + list(qreads), writes=[K.Bps[b]])
            if hasb:
                rhs, rds = T["bias"]
                P.op("pe", lambda e: e.matmul(K.ps[b][:, 0:256], ident[:], rhs, start=False, stop=True),
                     reads=[Bid] + list(rds), writes=[K.Bps[b]])
            k = st["npt"] % NPT
            st["npt"] += 1
            ptl[t] = k
            ab, abr = T["abias"]
            P.op("act", lambda e: e.activation(out=PT[k][:], in_=K.ps[b][:, 0:256], func=AF.Exp, bias=ab, scale=1.0),
                 reads=[K.Bps[b]] + list(abr), writes=[BPT[k]])

        def issue_pv(t):
            k = ptl[t]
            P.op("pe", lambda e: e.matmul(K.ps[bo][:, 0:256], vfn(t), PT[k][:], start=(t == 0), stop=(t == n - 1)),
                 reads=[BPT[k]] + list(vreads), writes=[K.Bps[bo]])

        LOOK = 2
        for t in range(min(LOOK, n)):
            issue_s(t)
        for t in range(n):
            if t + LOOK < n:
                issue_s(t + LOOK)
            issue_pv(t)
        if extra_den is not None:
            ed, edr = extra_den
            P.op("dve", lambda e: e.tensor_scalar(out=rc[64:128, :], in0=K.ps[bo][64:128, 0:256], scalar1=ed,
                                                  scalar2=None, op0=ALU.add),
                 reads=[K.Bps[bo]] + list(edr), writes=[Brc])
            P.op("dve", lambda e: e.reciprocal(out=rc[64:128, :], in_=rc[64:128, :]), reads=[Brc], writes=[Brc])
        else:
            P.op("dve", lambda e: e.reciprocal(out=rc[64:128, :], in_=K.ps[bo][64:128, 0:256]),
                 reads=[K.Bps[bo]], writes=[Brc])
        P.op("dve", lambda e: e.tensor_tensor(out=o_dst, in0=K.ps[bo][0:64, 0:256], in1=rc[64:128, :], op=ALU.mult),
             reads=[K.Bps[bo], Brc], writes=[o_buf])

    qst = P.sbuf("qst", [128, TPC], BF16)
    Bq = P.buf("qst")
    WS = P.sbuf("WS", [128, STRIPW], BF16)
    BWS = P.buf("WS")
    biasF = P.sbuf("biasF", [128, 128], F32)
    BbiasF = P.buf("biasF")

    def o_dest(fc, h, j):
        half = (h % 2) * 64
        return OT[half:half + 64, fc, j * 256:(j + 1) * 256]

    selw = P.sbuf("selw", [128, 128], BF16)
    Bselw = P.buf("selw")
    P.op("dve", lambda e: e.memset(selw[:], 0.0), writes=[Bselw])
    gm = P.sbuf("gm", [128, 64], F32)
    Bgm = P.buf("gm")
    m8 = P.sbuf("m8", [128, 8], F32)
    Bm8 = P.buf("m8")
    sa = P.sbuf("sa", [128, 64], F32)
    Bsa = P.buf("sa")
    for h in range(4):
        P.dma("sp", A1[0:64, :], kaT[h, :, :], writes=[BA1])
        P.dma("sp", A2v[:, :, 0:64], va[h, :, :, :], writes=[BA2])
        P.dma("sp", qst[0:64, :], qa[h, :, :], writes=[Bq])
        P.dma("sp", WS[:], strip[h, :, :], writes=[BWS])
        for tt in range(16):
            j = tt // 2
            bg = K.bank("s", SB)
            P.op("pe", lambda e, tt=tt, bg=bg, h=h: e.matmul(K.ps[bg][:, 0:64], qst[0:64, tt * 128:(tt + 1) * 128],
                                                             kmb[:, h, :], start=True, stop=True),
                 reads=[Bq, Bkmb], writes=[K.Bps[bg]])
            P.op("dve", lambda e, bg=bg, j=j: e.tensor_tensor(out=gm[:], in0=K.ps[bg][:, 0:64], in1=pm[:, j, :], op=ALU.add),
                 reads=[K.Bps[bg], Bpm], writes=[Bgm])
            P.op("dve", lambda e: e.max(out=m8[:], in_=gm[:]), reads=[Bgm], writes=[Bm8])
            P.op("dve", lambda e: e.tensor_scalar(out=sa[:], in0=gm[:], scalar1=m8[:, 2:3], scalar2=None, op0=ALU.is_ge),
                 reads=[Bgm, Bm8], writes=[Bsa])
            P.op("dve", lambda e: e.scalar_tensor_tensor(out=sa[:], in0=gm[:], scalar=-1e29, in1=sa[:],
                                                         op0=ALU.is_gt, op1=ALU.mult),
                 reads=[Bgm, Bsa], writes=[Bsa])
            P.op("dve", lambda e, j=j: e.tensor_tensor(out=sa[:], in0=sa[:], in1=own[:, j, :], op=ALU.add),
                 reads=[Bsa, Bown], writes=[Bsa])
            P.op("dve", lambda e: e.tensor_scalar(out=selw[:, 64:128], in0=sa[:], scalar1=-1.0, scalar2=-NEGM,
                                                  op0=ALU.add, op1=ALU.mult),
                 reads=[Bsa], writes=[Bselw])
            bt2 = K.bank("s", SB)
            P.op("pe", lambda e, bt2=bt2: e.matmul(K.ps[bt2][:, 0:128], selw[:], ident[:], start=True, stop=True),
                 reads=[Bselw, Bid], writes=[K.Bps[bt2]])
            P.op("act", lambda e, bt2=bt2, tt=tt: e.activation(out=qst[64:128, tt * 128:(tt + 1) * 128],
                                                               in_=K.ps[bt2][64:128, 0:128], func=AF.Copy),
                 reads=[K.Bps[bt2]], writes=[Bq])
        for j in range(8):
            tiles = []
            for t in range(16 * j + 16):
                i = 16 * j - t
                T = dict(lhsT=A1[:, t * 128:(t + 1) * 128], kreads=[BA1])
                if i <= IMAX:
                    T["bias"] = (WS[:, 128 * (i + 15):128 * (i + 15) + 256], [BWS])
                    T["abias"] = (zcol[:, 0:1], [K.Bconst2])
                else:
                    T["abias"] = (b31[:, h:h + 1], [Bb31])
                tiles.append(T)
            attn_chunk(tiles, qst[:, j * 256:(j + 1) * 256], [Bq], lambda t: A2v[:, t, :], [BA2],
                       o_dest(h // 2, h, j), BOT[h // 2], h)

    for h in range(4):
        P.dma("sp", A1[0:64, :], kbT[h, :, :], writes=[BA1])
        P.dma("sp", A2v[:, :, 0:64], vb[h, :, :, :], writes=[BA2])
        P.dma("sp", qst[0:64, :], qb[h, :, :], writes=[Bq])
        for j in range(8):
            P.op("dve", lambda e, j=j, h=h: e.tensor_scalar(out=biasF[:], in0=CLTv[:, h, :],
                                                            scalar1=clref[:, j * 4 + h:j * 4 + h + 1], scalar2=None,
                                                            op0=ALU.subtract),
                 reads=[BCLT, Bclref], writes=[BbiasF])
            tiles = []
            for t in range(16 * j + 16):
                T = dict(lhsT=A1[0:64, t * 128:(t + 1) * 128], kreads=[BA1], abias=(biasF[:, t:t + 1], [BbiasF]))
                if t >= 16 * j:
                    T["bias"] = (DM[:, t - 16 * j, :], [BDM])
                tiles.append(T)
            attn_chunk(tiles, qst[0:64, j * 256:(j + 1) * 256], [Bq], lambda t: A2v[:, t, :], [BA2],
                       o_dest(2 + h // 2, h, j), BOT[2 + h // 2], h)

    kcs = P.sbuf("kcs", [64, 8, 384], BF16)
    Bkcs = P.buf("kcs")
    vcs = P.sbuf("vcs", [128, 8, 3, 128], BF16)
    Bvcs = P.buf("vcs")
    P.op("dve", lambda e: e.memset(vcs[:, :, :, 64:128], 1.0), writes=[Bvcs])
    BCs = P.sbuf("BCs", [128, 3, 256], BF16)
    BBCs = P.buf("BCs")
    BC0s = P.sbuf("BC0s", [128, 256], BF16)
    BBC0s = P.buf("BC0s")
    for hq in range(8):
        kv = hq // 4
        if hq % 4 == 0:
            P.dma("sp", kcs[:], kch[kv, :, :, :], writes=[Bkcs])
            P.dma("sp", vcs[:, :, :, 0:64], vch[kv, :, :, :, :], writes=[Bvcs])
        P.dma("sp", qst[0:64, :], qc[hq, :, :], writes=[Bq])
        P.dma("sp", BCs[:], BCd[hq, :, :, :], writes=[BBCs])
        P.dma("sp", BC0s[:], BC0d[hq, :, :], writes=[BBC0s])
        for j in range(8):
            tiles = []
            for i in range(3):
                T = dict(lhsT=kcs[:, j, i * 128:(i + 1) * 128], kreads=[Bkcs], abias=(zcol[:, 0:1], [K.Bconst2]))
                if j == 0 and i == 0:
                    T["bias"] = (BC0s[:], [BBC0s])
                else:
                    T["bias"] = (BCs[:, i, :], [BBCs])
                tiles.append(T)
            attn_chunk(tiles, qst[0:64, j * 256:(j + 1) * 256], [Bq], lambda t, j=j: vcs[:, j, t, :], [Bvcs],
                       o_dest(4 + hq // 2, hq, j), BOT[4 + hq // 2], hq,
                       extra_den=(esink[64:128, hq:hq + 1], [Besink]))

    ot_out = K.dout("OTo", [D, TPC], BF16)
    if K_DEBUG_OT:
        for fc in range(8):
            P.dma("sp", ot_out[fc * 128:(fc + 1) * 128, :], OT[:, fc, :], reads=[BOT[fc]], final=True)

    K.sq = WS[:, 0:4096].rearrange("p (kc t) -> p kc t", kc=8)
    K.Bsq = BWS
    K.rstd = P.sbuf("rstd", [128, 512], F32)
    K.Brstd = P.buf("rstd")
    K.rstd2 = P.sbuf("rstd2", [128, 512], F32)
    K.Brstd2 = P.buf("rstd2")
    ln1, Bln1 = load_small(K, "ln1", [128, 8])
    xin = K.din("xT", [D, TPC], F32).rearrange("(kc p) t -> p kc t", p=128)
    xout = K.dout("xTm", [D, TPC], F32).rearrange("(kc p) t -> p kc t", p=128)
    wg = K.din("wg", [D, 3072], F32).rearrange("(kc p) n -> p kc n", p=128)
    wp = K.din("wp", [D, D], F32).rearrange("(kc p) n -> p kc n", p=128)
    wo = K.din("wo", [D, D], F32).rearrange("(kc p) n -> p kc n", p=128)
    hTt = P.sbuf("hTt", [128, 8, 512], BF16)
    BhTt = P.buf("hTt")
    mg = P.sbuf("mg", [128, 8, 512], BF16)
    Bmg = P.bufs(8, "mg")
    wgs = [P.sbuf(f"wgs{i}", [128, 8, 128], BF16) for i in range(3)]
    Bwgs = P.bufs(3, "wgs")
    wps = [P.sbuf(f"wps{i}", [128, 8, 128], BF16) for i in range(2)]
    Bwps = P.bufs(2, "wps")
    wos = [P.sbuf(f"wos{i}", [128, 8, 128], BF16) for i in range(2)]
    Bwos = P.bufs(2, "wos")
    sig = P.sbuf("sig", [128, 512], F32)
    Bsig = P.buf("sig")
    acc = P.sbuf("acc", [128, 512], F32)
    Bacc = P.buf("acc")
    tmp = P.sbuf("tmp", [128, 512], F32)
    Btmp = P.buf("tmp")
    nwg = 0
    for tg in range(4):
        sl = slice(tg * 512, (tg + 1) * 512)
        P.dma("sp", K.xT[:, :, sl], xin[:, :, sl],
              writes=[K.BxT[ft][tg] for ft in range(8)] + [BA1, BA2], sembuf=K.BxT[0][tg])
        rmsnorm_tg(K, tg, ln1, Bln1, hTt, BhTt, slice(0, 512))
        for m in range(8):
            wpt = wps[m % 2]
            bwp = Bwps[m % 2]
            P.dma("pool", wpt[:], wp[:, :, m * 128:(m + 1) * 128], writes=[bwp])
            for br in range(3):
                w = wgs[nwg % 3]
                bw_ = Bwgs[nwg % 3]
                nwg += 1
                P.dma("pool", w[:], wg[:, :, br * 1024 + m * 128:br * 1024 + (m + 1) * 128], writes=[bw_])
                bgt = K.bank()
                for kc in range(8):
                    P.op("pe", lambda e, kc=kc, w=w, bgt=bgt: e.matmul(K.ps[bgt][:], w[:, kc, :], hTt[:, kc, :],
                                                                        start=(kc == 0), stop=(kc == 7)),
                         reads=[bw_, BhTt], writes=[K.Bps[bgt]])
                fcs = ((0, 1), (2, 3), (4, 5, 6, 7))[br]
                bpj = K.bank()
                for ii, fc in enumerate(fcs):
                    P.op("pe", lambda e, fc=fc, ii=ii, bpj=bpj, wpt=wpt, n=len(fcs), sl=sl: e.matmul(
                        K.ps[bpj][:], wpt[:, fc, :], OT[:, fc, sl], start=(ii == 0), stop=(ii == n - 1)),
                        reads=[bwp, BOT[fc]], writes=[K.Bps[bpj]])
                P.op("act", lambda e, bgt=bgt: e.activation(out=sig[:], in_=K.ps[bgt][:], func=AF.Sigmoid),
                     reads=[K.Bps[bgt]], writes=[Bsig])
                if br == 0:
                    P.op("dve", lambda e, bpj=bpj: e.tensor_tensor(out=acc[:], in0=K.ps[bpj][:], in1=sig[:], op=ALU.mult),
                         reads=[K.Bps[bpj], Bsig], writes=[Bacc])
                else:
                    P.op("dve", lambda e, bpj=bpj: e.tensor_tensor(out=tmp[:], in0=K.ps[bpj][:], in1=sig[:], op=ALU.mult),
                         reads=[K.Bps[bpj], Bsig], writes=[Btmp])
                    if br == 1:
                        P.op("dve", lambda e: e.tensor_tensor(out=acc[:], in0=acc[:], in1=tmp[:], op=ALU.add),
                             reads=[Bacc, Btmp], writes=[Bacc])
                    else:
                        P.op("dve", lambda e, m=m: e.tensor_tensor(out=mg[:, m, :], in0=acc[:], in1=tmp[:], op=ALU.add),
                             reads=[Bacc, Btmp], writes=[Bmg[m]])
        for mo in range(8):
            w = wos[mo % 2]
            bwo = Bwos[mo % 2]
            P.dma("pool", w[:], wo[:, :, mo * 128:(mo + 1) * 128], writes=[bwo])
            bb = K.bank()
            for kc in range(8):
                P.op("pe", lambda e, kc=kc, w=w, bb=bb: e.matmul(K.ps[bb][:], w[:, kc, :], mg[:, kc, :],
                                                                  start=(kc == 0), stop=(kc == 7)),
                     reads=[bwo, Bmg[kc]], writes=[K.Bps[bb]])
            P.op("dve", lambda e, mo=mo, bb=bb, sl=sl: e.tensor_tensor(out=K.xT[:, mo, sl], in0=K.xT[:, mo, sl],
                                                                       in1=K.ps[bb][:], op=ALU.add),
                 reads=[K.Bps[bb], K.BxT[mo][tg]], writes=[K.BxT[mo][tg]])
        P.dma("sp", xout[:, :, sl], K.xT[:, :, sl], reads=[K.BxT[ft][tg] for ft in range(8)],
              sembuf=K.BxT[1][tg], final=True)
    return K


K_DEBUG_OT = True


def t5_bucket_np(d):
    d = np.maximum(np.asarray(d, dtype=np.int64), 0)
    df = np.maximum(d.astype(np.float32), np.float32(1.0))
    large = 16 + (np.log(df / np.float32(16.0)) / np.float32(math.log(4096 / 16)) * np.float32(16.0)).astype(np.int32)
    large = np.minimum(large, 31)
    return np.where(d < 16, d, large).astype(np.int64)


_CONST = {}


def core_consts(c):
    if c in _CONST:
        return _CONST[c]
    p = np.arange(128)
    r = {}
    r["ident"] = np.eye(128, dtype=np.float32).astype(NPBF)
    r["utri"] = (p[:, None] <= p[None, :]).astype(np.float32)
    OH = np.zeros((128, 8, 128), np.float32)
    pm = np.zeros((128, 8, 64), np.float32)
    own = np.zeros((128, 8, 64), np.float32)
    for j in range(8):
        g = 8 * j + c
        OH[:, j, 2 * g] = 1.0
        pm[:, j, g:] = -1e30
        own[:, j, g] = 1.0
    r["OH"], r["pm"], r["own"] = OH, pm, own
    i = np.arange(16)[None, :, None]
    ql = np.arange(256)[None, None, :]
    kpos = 128 * i + p[:, None, None]
    r["DM"] = np.where(kpos <= 256 * c + ql, 0.0, NEGM).astype(NPBF)
    r["ind"] = (np.arange(S)[None, :] // CH == np.arange(64)[:, None]).astype(np.float32).astype(NPBF)
    x = np.arange(STRIPW)[None, :]
    d = x - 1920 + 256 * c - p[:, None]
    r["strip_idx"] = t5_bucket_np(d)
    r["strip_neg"] = d < 0
    i3 = np.arange(3)[None, :, None]
    dc = ql - 128 * i3 - p[:, None, None] + 128
    r["bc_idx"] = t5_bucket_np(dc)
    r["bc_neg"] = (dc < 0) | (dc >= 128)
    _CONST[c] = r
    return r


def ka_inputs(l, c, inp, g):
    cc = core_consts(c)
    m = {k: cc[k] for k in ("ident", "utri", "OH", "pm", "own", "DM", "ind")}
    rb = inp["rel_bias"]
    st = np.empty((4, 128, STRIPW), np.float32)
    for h in range(4):
        st[h] = np.where(cc["strip_neg"], np.float32(NEGM), rb[cc["strip_idx"], h])
    m["strip"] = st.astype(NPBF)
    m["b31"] = np.ascontiguousarray(np.broadcast_to(rb[31, 0:4][None, :], (128, 4)))
    bc = np.empty((8, 128, 3, 256), np.float32)
    for hq in range(8):
        bc[hq] = np.where(cc["bc_neg"], np.float32(NEGM), rb[cc["bc_idx"], 4 + hq])
    m["BC"] = bc.astype(NPBF)
    bc0 = bc[:, :, 0, :].copy()
    if c == 0:
        bc0[:] = NEGM
    m["BC0"] = bc0.astype(NPBF)
    m["sinkb"] = np.ascontiguousarray(np.broadcast_to(inp["sinks"][l][None, :], (128, 8)))
    tp = tok_perm(c)
    m["qa"] = np.ascontiguousarray(g["qT"][0:256][:, tp].reshape(4, 64, TPC))
    m["qb"] = np.ascontiguousarray(g["qT"][256:512][:, tp].reshape(4, 64, TPC))
    m["qc"] = np.ascontiguousarray(g["qT"][512:1024][:, tp].reshape(8, 64, TPC))
    m["kaT"] = g["kaT"]
    m["kbT"] = g["kbT"]
    m["va"] = g["va"]
    m["vb"] = g["vb"]
    kch = np.zeros((2, 64, 8, 384), NPBF)
    vch = np.zeros((2, 128, 8, 3, 64), NPBF)
    for j in range(8):
        g0 = CH * (8 * j + c)
        lo = g0 - 128
        if lo >= 0:
            kch[:, :, j, :] = g["kcT"][:, lo:lo + 384].reshape(2, 64, 384)
            vw = g["vc"][lo:lo + 384]
        else:
            kch[:, :, j, 128:] = g["kcT"][:, 0:256].reshape(2, 64, 256)
            vw = np.concatenate([np.zeros((128, 128), NPBF), g["vc"][0:256]], 0)
        vch[:, :, j, :, :] = vw.reshape(3, 128, 2, 64).transpose(2, 1, 0, 3)
    m["kch"], m["vch"] = kch, vch
    m["kmT"] = g["kmT"]
    m["lfT"] = g["lfT"]
    m["ln1"] = np.ascontiguousarray(inp["ln1"][l].reshape(8, 128).T)
    w = inp["w_in"][l]
    m["wg"] = np.ascontiguousarray(w[:, O_GA:O_GA + 3072])
    m["wp"] = np.ascontiguousarray(np.concatenate([inp["w_pa"][l], inp["w_pb"][l], inp["w_pc"][l]], 0))
    m["wo"] = np.ascontiguousarray(inp["w_o"][l])
    return m


def gather_p1(res):
    qk = np.empty((NFM * 128, S), NPBF)
    v = np.empty((S, 640), NPBF)
    kmT = np.empty((64, 4, 64), np.float32)
    lfT = np.empty((128, 128, 4), np.float32)
    for c in range(NCORE):
        tp = tok_perm(c)
        qk[:, tp] = np.asarray(res[c]["qkT"])
        v[tp] = np.asarray(res[c]["vtm"])
        kms = np.asarray(res[c]["kms"]).reshape(2, 64, 2, 8)
        for j in range(8):
            kmT[:, :, 8 * j + c] = kms[:, :, :, j].transpose(1, 2, 0).reshape(64, 4)
        lf = np.asarray(res[c]["lf"]).reshape(128, 16, 4)
        for j in range(8):
            for half in range(2):
                lfT[:, 2 * (8 * j + c) + half, :] = lf[:, 2 * j + half, :]
    g = {}
    g["qT"] = np.concatenate([qk[0:256], qk[512:768], qk[1024:1536]], 0)
    g["kaT"] = np.ascontiguousarray(qk[256:512].reshape(4, 64, S))
    g["kbT"] = np.ascontiguousarray(qk[768:1024].reshape(4, 64, S))
    g["kcT"] = np.ascontiguousarray(qk[1536:1664])
    g["va"] = np.ascontiguousarray(v[:, 0:256].reshape(128, 128, 4, 64).transpose(2, 1, 0, 3))
    g["vb"] = np.ascontiguousarray(v[:, 256:512].reshape(128, 128, 4, 64).transpose(2, 1, 0, 3))
    g["vc"] = np.ascontiguousarray(v[:, 512:640])
    g["kmT"] = kmT
    g["lfT"] = np.ascontiguousarray(lfT.reshape(128, 512))
    return g


def build_kb(last=False):
    K = KB("kbl" if last else "kb")
    P = K.P
    setup_common(K)
    setup_eps(K)
    setup_onecol(K)
    ln2, Bln2 = load_small(K, "ln2", [128, 8])
    cw, Bcw = load_small(K, "cw", [128, 44, 3])
    cb, Bcb = load_small(K, "cb", [128, 44])
    xh, Bxh = load_small(K, "xh", [128, 8, 16])
    wup = K.din("wup", [D, 2 * DFF], F32).rearrange("(kc p) n -> p kc n", p=128)
    wdn = K.din("wdn", [DFF, D], F32).rearrange("(kc p) n -> p kc n", p=128)
    sqh = P.sbuf("sqh", [128, 8, 16], BF16)
    Bsqh = P.buf("sqh")
    P.op("act", lambda e: e.activation(out=sqh[:], in_=xh[:], func=AF.Square), reads=[Bxh], writes=[Bsqh])
    b = K.bank()
    for kc in range(8):
        P.op("pe", lambda e, kc=kc: e.matmul(K.ps[b][:, 0:16], K.onesb[:], sqh[:, kc, :], start=(kc == 0), stop=(kc == 7)),
             reads=[Bsqh, K.Bconst], writes=[K.Bps[b]])
    rsh = P.sbuf("rsh", [128, 16], F32)
    Brsh = P.buf("rsh")
    rsh2 = P.sbuf("rsh2", [128, 16], F32)
    Brsh2 = P.buf("rsh2")
    P.op("act", lambda e: e.activation(out=rsh2[:], in_=K.ps[b][:, 0:16], func=AF.Sqrt, bias=K.epscol[:, 0:1],
                                        scale=1.0 / D), reads=[K.Bps[b], K.Bconst2], writes=[Brsh2])
    P.op("dve", lambda e: e.reciprocal(out=rsh[:], in_=rsh2[:]), reads=[Brsh2], writes=[Brsh])
    h2h = P.sbuf("h2h", [128, 8, 16], BF16)
    Bh2h = P.buf("h2h")
    for kc in range(8):
        P.op("dve", lambda e, kc=kc: e.scalar_tensor_tensor(out=h2h[:, kc, :], in0=xh[:, kc, :], scalar=ln2[:, kc:kc + 1],
                                                            in1=rsh[:], op0=ALU.mult, op1=ALU.mult),
             reads=[Bxh, Brsh, Bln2], writes=[Bh2h])

    h2T = P.sbuf("h2T", [128, 8, 512], BF16)
    Bh2 = P.buf("h2T")
    actT = K.A1[:, 0:22 * 512].rearrange("p (m t) -> p m t", m=22)
    Bact = P.bufs(22, "act")
    U = [P.sbuf(f"U{i}", [128, 2, 258], F32) for i in range(2)]
    BU = P.bufs(2, "U")
    cv = [P.sbuf(f"cv{i}", [128, 2, 256], F32) for i in range(2)]
    Bcv = P.bufs(2, "cv")
    sil = P.sbuf("sil", [128, 2, 256], F32)
    Bsil = P.buf("sil")
    wu = [P.sbuf(f"wu{i}", [128, 8, 128], BF16) for i in range(4)]
    Bwu = P.bufs(4, "wu")
    wd = [P.sbuf(f"wd{i}", [128, 22, 128], BF16) for i in range(2)]
    Bwd = P.bufs(2, "wd")
    nw = 0
    for tg in range(4):
        sl = slice(tg * 512, (tg + 1) * 512)
        rmsnorm_tg(K, tg, ln2, Bln2, h2T, Bh2, slice(0, 512))
        for m in range(22):
            for part, ct in ((0, m), (1, 22 + m)):
                w = wu[nw % 4]
                bw = Bwu[nw % 4]
                nw += 1
                P.dma("pool", w[:], wup[:, :, ct * 128:(ct + 1) * 128], writes=[bw])
                bm = K.bank()
                for kc in range(8):
                    P.op("pe", lambda e, kc=kc, w=w, bm=bm: e.matmul(K.ps[bm][:], w[:, kc, :], h2T[:, kc, :],
                                                                      start=(kc == 0), stop=(kc == 7)),
                         reads=[bw, Bh2], writes=[K.Bps[bm]])
                bh = K.bank()
                for kc in range(8):
                    P.op("pe", lambda e, kc=kc, w=w, bh=bh, tg=tg: e.matmul(K.ps[bh][:, 0:4], w[:, kc, :],
                                                                             h2h[:, kc, 4 * tg:4 * tg + 4],
                                                                             start=(kc == 0), stop=(kc == 7)),
                         reads=[bw, Bh2h], writes=[K.Bps[bh]])
                Up = U[part]
                P.op("act", lambda e, Up=Up, bm=bm: e.activation(out=Up[:, :, 2:258],
                                                                 in_=K.ps[bm][:].rearrange("p (c t) -> p c t", c=2),
                                                                 func=AF.Copy),
                     reads=[K.Bps[bm]], writes=[BU[part]])
                P.op("dve", lambda e, Up=Up, bh=bh: e.tensor_copy(out=Up[:, :, 0:2],
                                                                  in_=K.ps[bh][:, 0:4].rearrange("p (c t) -> p c t", c=2)),
                     reads=[K.Bps[bh]], writes=[BU[part]])
                cvp = cv[part]
                P.op("act", lambda e, Up=Up, cvp=cvp, ct=ct: e.activation(out=cvp[:], in_=Up[:, :, 2:258], func=AF.Identity,
                                                                          bias=cb[:, ct:ct + 1], scale=cw[:, ct, 2:3]),
                     reads=[BU[part], Bcw, Bcb], writes=[Bcv[part]])
                P.op("dve", lambda e, Up=Up, cvp=cvp, ct=ct: e.scalar_tensor_tensor(
                    out=cvp[:], in0=Up[:, :, 1:257], scalar=cw[:, ct, 1:2], in1=cvp[:], op0=ALU.mult, op1=ALU.add),
                    reads=[BU[part], Bcw, Bcv[part]], writes=[Bcv[part]])
                P.op("dve", lambda e, Up=Up, cvp=cvp, ct=ct: e.scalar_tensor_tensor(
                    out=cvp[:], in0=Up[:, :, 0:256], scalar=cw[:, ct, 0:1], in1=cvp[:], op0=ALU.mult, op1=ALU.add),
                    reads=[BU[part], Bcw, Bcv[part]], writes=[Bcv[part]])
            P.op("act", lambda e: e.activation(out=sil[:], in_=cv[0][:], func=AF.Silu), reads=[Bcv[0]], writes=[Bsil])
            P.op("dve", lambda e, m=m: e.tensor_tensor(out=actT[:, m, :], in0=sil[:].rearrange("p c t -> p (c t)"),
                                                       in1=cv[1][:].rearrange("p c t -> p (c t)"), op=ALU.mult),
                 reads=[Bsil, Bcv[1]], writes=[Bact[m]])
        for mo in range(8):
            w = wd[mo % 2]
            bwd = Bwd[mo % 2]
            P.dma("pool", w[:], wdn[:, :, mo * 128:(mo + 1) * 128], writes=[bwd])
            bb = K.bank()
            for kc in range(22):
                P.op("pe", lambda e, kc=kc, w=w, bb=bb: e.matmul(K.ps[bb][:], w[:, kc, :], actT[:, kc, :],
                                                                  start=(kc == 0), stop=(kc == 21)),
                     reads=[bwd, Bact[kc]], writes=[K.Bps[bb]])
            P.op("dve", lambda e, mo=mo, bb=bb, sl=sl: e.tensor_tensor(out=K.xT[:, mo, sl], in0=K.xT[:, mo, sl],
                                                                       in1=K.ps[bb][:], op=ALU.add),
                 reads=[K.Bps[bb], K.BxT[mo][tg]], writes=[K.BxT[mo][tg]])
    if not last:
        xout = K.dout("xTo", [D, TPC], F32).rearrange("(kc p) t -> p kc t", p=128)
        for tg in range(4):
            sl = slice(tg * 512, (tg + 1) * 512)
            P.dma("sp", xout[:, :, sl], K.xT[:, :, sl], reads=[K.BxT[ft][tg] for ft in range(8)],
                  sembuf=K.BxT[1][tg], final=True)
        phase_p1(K, sfx="", extra_w=Bact)
    else:
        lnf, Blnf = load_small(K, "lnf", [128, 8])
        yout = K.dout("yT", [D, TPC], F32).rearrange("(kc p) t -> p kc t", p=128)
        yst = P.sbuf("yst", [128, 8, 512], F32)
        Byst = P.buf("yst")
        for tg in range(4):
            sl = slice(tg * 512, (tg + 1) * 512)
            rmsnorm_tg(K, tg, lnf, Blnf, yst, Byst, slice(0, 512))
            P.dma("sp", yout[:, :, sl], yst[:], reads=[Byst], final=True)
    return K


def build_kbl():
    return build_kb(last=True)


def kb_inputs(l, c, inp, xmid_g, last):
    m = {}
    m["ln2"] = np.ascontiguousarray(inp["ln2"][l].reshape(8, 128).T)
    m["cw"] = np.ascontiguousarray(inp["conv_w"][l].reshape(3, 44, 128).transpose(2, 1, 0))
    m["cb"] = np.ascontiguousarray(inp["conv_b"][l].reshape(44, 128).T)
    xh = np.zeros((16, D), np.float32)
    for j in range(8):
        g0 = CH * (8 * j + c)
        if g0 > 0:
            xh[2 * j:2 * j + 2] = xmid_g[g0 - 2:g0]
    m["xh"] = np.ascontiguousarray(xh.T.reshape(8, 128, 16).transpose(1, 0, 2))
    m["wup"] = inp["w_up"][l]
    m["wdn"] = inp["w_down"][l]
    if last:
        m["lnf"] = np.ascontiguousarray(inp["ln_f"].reshape(8, 128).T)
    else:
        m.update(p1_inputs(l + 1, inp))
    return m


def gather_tok(res, key):
    out = np.empty((S, D), np.float32)
    for c in range(NCORE):
        out[tok_perm(c)] = np.asarray(res[c][key]).T
    return out


def kernel(**inputs):
    inp = {k: np.asarray(v) for k, v in inputs.items()}
    x = inp["x"][0]
    K0 = get_kernel("k0", build_k0)
    KA = get_kernel("ka", build_ka)
    KBn = get_kernel("kb", build_kb)
    KBl = get_kernel("kbl", build_kbl)
    xT = [np.ascontiguousarray(x[tok_perm(c)].T) for c in range(NCORE)]
    pi = p1_inputs(0, inp)
    res1 = run(K0, [dict(pi, xT=xT[c]) for c in range(NCORE)])
    for l in range(DEPTH):
        g = gather_p1(res1)
        maps = []
        for c in range(NCORE):
            m = ka_inputs(l, c, inp, g)
            m["xT"] = xT[c]
            maps.append(m)
        resa = run(KA, maps)
        xmid_g = gather_tok(resa, "xTm")
        last = (l == DEPTH - 1)
        maps = []
        for c in range(NCORE):
            m = kb_inputs(l, c, inp, xmid_g, last)
            m["xT"] = np.asarray(resa[c]["xTm"])
            maps.append(m)
        resb = run(KBl if last else KBn, maps)
        if last:
            y = gather_tok(resb, "yT")
            return y.reshape(1, S, D).astype(np.float32)
        xT = [np.asarray(resb[c]["xTo"]) for c in range(NCORE)]
        res1 = resb
```
